# Optimizing a Trainium2 kernel written in Bass

```python
import jax, jax.numpy as jnp
from jax import lax
import numpy as np


D_MODEL = 1024
BATCH = 16
SEQ = 4096
DEPTH = 1

PLE_DIM = 256
M_HEADS = 4
M_HEAD_DIM = 128
M_WIDTH = M_HEADS * M_HEAD_DIM
M_CHUNK = 64
A_HEADS = 8
A_KV_GROUPS = 2
A_HEAD_DIM = 64
A_WIDTH = A_HEADS * A_HEAD_DIM
KV_WIDTH = A_KV_GROUPS * A_HEAD_DIM
CMP_BLOCK = 32
CMP_STRIDE = 16
CMP_HIDDEN = 128
SEL_BLOCK = 64
N_SELECT = 16
WINDOW = 512
A_Q_BLOCK = 32
ALIBI_MAX = 8.0
MIX_WIDTH = M_WIDTH + A_WIDTH
IN_WIDTH = 4 * M_WIDTH + 2 * M_HEADS + A_WIDTH + 6 * KV_WIDTH + 3 * A_HEADS
D_FF = 2048
CONV_WIDTH = 3
EPS = 1e-6
NEG = -1e30
BIG = 1e30

kernel_name = "hymba_mlstm_nsa_convffn_ple"


def rms_norm(x, g):
    xf = x.astype(jnp.float32)
    y = xf * lax.rsqrt(jnp.mean(xf * xf, axis=-1, keepdims=True) + EPS)
    return (y * g.astype(jnp.float32)).astype(x.dtype)


def alibi_slopes(n):
    return jnp.exp2(-ALIBI_MAX * jnp.arange(1, n + 1, dtype=jnp.float32) / n)


def mlstm(q, k, v, i_pre, f_pre):
    B, S, H, dh = q.shape
    L = M_CHUNK
    NC = S // L
    f32 = jnp.float32

    def chunks(t):
        t = t.astype(f32).reshape((B, NC, L, H) + t.shape[3:])
        return jnp.moveaxis(t, 3, 1)

    qc = chunks(q) * (dh ** -0.5)
    kc = chunks(k)
    vc = chunks(v)
    ig = chunks(i_pre)
    b = jnp.cumsum(jax.nn.log_sigmoid(chunks(f_pre)), axis=-1)
    bL = b[..., -1]
    a = bL[..., None] - b + ig

    def step(carry, inp):
        C, n, m = carry
        k_, v_, a_, bL_ = inp
        m_new = jnp.maximum(bL_ + m, jnp.max(a_, axis=-1))
        decay = jnp.exp(bL_ + m - m_new)
        w = jnp.exp(a_ - m_new[..., None])
        C_new = decay[..., None, None] * C + jnp.einsum('bhl,bhlk,bhlv->bhkv', w, k_, v_)
        n_new = decay[..., None] * n + jnp.einsum('bhl,bhlk->bhk', w, k_)
        return (C_new, n_new, m_new), (C, n, m)

    init = (jnp.zeros((B, H, dh, dh), f32), jnp.zeros((B, H, dh), f32), jnp.zeros((B, H), f32))
    xs = (jnp.moveaxis(kc, 2, 0), jnp.moveaxis(vc, 2, 0), jnp.moveaxis(a, 2, 0), jnp.moveaxis(bL, 2, 0))
    _, (Cs, ns, ms) = lax.scan(step, init, xs)
    Cs = jnp.moveaxis(Cs, 0, 2)
    ns = jnp.moveaxis(ns, 0, 2)
    ms = jnp.moveaxis(ms, 0, 2)

    bq = b + ms[..., None]
    causal = jnp.tril(jnp.ones((L, L), dtype=bool))
    Dm = jnp.where(causal, b[..., :, None] - b[..., None, :] + ig[..., None, :], -jnp.inf)
    m = jnp.maximum(bq, jnp.max(Dm, axis=-1))
    inter = jnp.exp(bq - m)
    Wt = jnp.exp(Dm - m[..., None]) * jnp.einsum('bhcld,bhcsd->bhcls', qc, kc)
    num = inter[..., None] * jnp.einsum('bhcld,bhcdv->bhclv', qc, Cs) + jnp.einsum('bhcls,bhcsv->bhclv', Wt, vc)
    den = inter * jnp.einsum('bhcld,bhcd->bhcl', qc, ns) + jnp.sum(Wt, axis=-1)
    h = num / jnp.maximum(jnp.abs(den), jnp.exp(-m))[..., None]
    return jnp.moveaxis(h, 1, 3).reshape(B, S, H, dh)


def nsa(q, kc_raw, vc_raw, ks_raw, vs_raw, kw_raw, vw_raw, g_pre, pe_k, pe_v, ck_w1, ck_w2, cv_w1, cv_w2):
    f32 = jnp.float32
    B, S, H, dh = q.shape
    G = A_KV_GROUPS
    R = H // G
    qg = q.astype(f32).reshape(B, S, G, R, dh).transpose(0, 2, 3, 1, 4) * (dh ** -0.5)
    gates = jax.nn.sigmoid(g_pre.astype(f32).reshape(B, S, G, R, 3)).transpose(0, 2, 3, 1, 4)
    to_g = lambda t: t.astype(f32).transpose(0, 2, 1, 3)

    Nc = (S - CMP_BLOCK) // CMP_STRIDE + 1
    cidx = jnp.arange(Nc)[:, None] * CMP_STRIDE + jnp.arange(CMP_BLOCK)[None, :]

    def compress(t, pe, w1, w2):
        blk = to_g(t)[:, :, cidx] + pe.astype(f32)
        flat = blk.reshape(B, G, Nc, CMP_BLOCK * dh)
        return jax.nn.gelu(flat @ w1.astype(f32)) @ w2.astype(f32)

    Kc = compress(kc_raw, pe_k, ck_w1, ck_w2)
    Vc = compress(vc_raw, pe_v, cv_w1, cv_w2)
    cmp_start = jnp.arange(Nc) * CMP_STRIDE
    cmp_end = cmp_start + CMP_BLOCK - 1

    Ns = S // SEL_BLOCK
    n_sel = min(N_SELECT, Ns)
    Ks = to_g(ks_raw).reshape(B, G, Ns, SEL_BLOCK, dh)
    Vs = to_g(vs_raw).reshape(B, G, Ns, SEL_BLOCK, dh)
    sel_start = jnp.arange(Ns) * SEL_BLOCK
    overlap = ((cmp_start[:, None] <= sel_start[None, :] + SEL_BLOCK - 1)
               & (cmp_end[:, None] >= sel_start[None, :])).astype(f32)

    pad = ((0, 0), (0, 0), (WINDOW, 0), (0, 0))
    Kw = jnp.pad(to_g(kw_raw), pad)
    Vw = jnp.pad(to_g(vw_raw), pad)

    slope = alibi_slopes(H).reshape(1, G, R, 1, 1)
    bi = jnp.arange(B)[:, None, None, None]
    gi = jnp.arange(G)[None, :, None, None]
    T = A_Q_BLOCK

    def block(qb):
        t0 = qb * T
        t = t0 + jnp.arange(T)
        qblk = lax.dynamic_slice_in_dim(qg, t0, T, axis=3)
        gblk = lax.dynamic_slice_in_dim(gates, t0, T, axis=3)

        valid_c = cmp_end[None, :] <= t[:, None]
        dist_c = (t[:, None] - cmp_end[None, :]).astype(f32)
        s_c = jnp.einsum('bgrtd,bgnd->bgrtn', qblk, Kc) - slope * dist_c
        p_c = jnp.where(valid_c, jax.nn.softmax(jnp.where(valid_c, s_c, NEG), axis=-1), 0.0)
        o_c = jnp.einsum('bgrtn,bgnd->bgrtd', p_c, Vc)

        imp = jnp.einsum('bgrtn,nj->bgtj', p_c, overlap)
        j = jnp.arange(Ns)[None, :]
        cur = (t // SEL_BLOCK)[:, None]
        forced = (j == 0) | (j == cur) | (j == cur - 1)
        causal_b = sel_start[None, :] <= t[:, None]
        score = jnp.where(causal_b, jnp.where(forced, BIG, imp), NEG)
        sel_idx = lax.top_k(score, n_sel)[1]
        Ksel = Ks[bi, gi, sel_idx]
        Vsel = Vs[bi, gi, sel_idx]
        pos_s = sel_idx[..., None] * SEL_BLOCK + jnp.arange(SEL_BLOCK)
        dist_s = (t[None, None, :, None, None] - pos_s)
        valid_s = (dist_s >= 0)[:, :, None]
        s_s = jnp.einsum('bgrtd,bgtnld->bgrtnl', qblk, Ksel) - slope[..., None] * dist_s[:, :, None].astype(f32)
        s_s = jnp.where(valid_s, s_s, NEG).reshape(B, G, R, T, n_sel * SEL_BLOCK)
        p_s = jax.nn.softmax(s_s, axis=-1).reshape(B, G, R, T, n_sel, SEL_BLOCK)
        o_s = jnp.einsum('bgrtnl,bgtnld->bgrtd', p_s, Vsel)

        kwin = lax.dynamic_slice_in_dim(Kw, t0, WINDOW + T, axis=2)
        vwin = lax.dynamic_slice_in_dim(Vw, t0, WINDOW + T, axis=2)
        pos_w = t0 - WINDOW + jnp.arange(WINDOW + T)
        dist_w = t[:, None] - pos_w[None, :]
        valid_w = (pos_w[None, :] >= 0) & (dist_w >= 0) & (dist_w < WINDOW)
        s_w = jnp.einsum('bgrtd,bgsd->bgrts', qblk, kwin) - slope * dist_w.astype(f32)
        p_w = jax.nn.softmax(jnp.where(valid_w, s_w, NEG), axis=-1)
        o_w = jnp.einsum('bgrts,bgsd->bgrtd', p_w, vwin)

        return gblk[..., 0:1] * o_c + gblk[..., 1:2] * o_s + gblk[..., 2:3] * o_w

    outs = lax.map(block, jnp.arange(S // T))
    return outs.transpose(1, 0, 4, 2, 3, 5).reshape(B, S, H * dh)


def conv_ffn(h, w_up, conv_w, conv_b, w_down):
    u = h @ w_up
    gate, val = jnp.split(u, 2, axis=-1)
    gate = lax.conv_general_dilated(gate, conv_w[:, None, :].astype(gate.dtype), window_strides=(1,),
                                    padding=[(CONV_WIDTH - 1, 0)],
                                    dimension_numbers=('NWC', 'WIO', 'NWC'),
                                    feature_group_count=D_FF) + conv_b
    return (jax.nn.silu(gate) * val) @ w_down


def setup_inputs(seed: int = 0) -> dict:
    key = jax.random.key(seed)
    ks = jax.random.split(key, 24)
    f32 = jnp.float32
    nrm = lambda k, shape, scale: jax.random.normal(k, shape, f32) * scale
    gain = lambda k, shape: 1.0 + 0.05 * jax.random.normal(k, shape, f32)
    i_bias = nrm(ks[4], (DEPTH, M_HEADS), 0.1)
    f_bias = jnp.linspace(3.0, 6.0, M_HEADS, dtype=f32)[None, :] + nrm(ks[5], (DEPTH, M_HEADS), 0.1)
    return {
        "x": nrm(ks[0], (BATCH, SEQ, D_MODEL), 1.0),
        "p": nrm(ks[1], (DEPTH, BATCH, SEQ, PLE_DIM), 1.0),
        "ln1_g": gain(ks[2], (DEPTH, D_MODEL)),
        "w_in": nrm(ks[3], (DEPTH, D_MODEL, IN_WIDTH), D_MODEL ** -0.5),
        "mlstm_gate_bias": jnp.concatenate([i_bias, f_bias], axis=-1),
        "mlstm_norm_g": gain(ks[6], (DEPTH, M_WIDTH)),
        "cmp_pos_k": nrm(ks[7], (DEPTH, CMP_BLOCK, A_HEAD_DIM), 0.02),
        "cmp_pos_v": nrm(ks[8], (DEPTH, CMP_BLOCK, A_HEAD_DIM), 0.02),
        "cmp_k_w1": nrm(ks[9], (DEPTH, CMP_BLOCK * A_HEAD_DIM, CMP_HIDDEN), (CMP_BLOCK * A_HEAD_DIM) ** -0.5),
        "cmp_k_w2": nrm(ks[10], (DEPTH, CMP_HIDDEN, A_HEAD_DIM), CMP_HIDDEN ** -0.5),
        "cmp_v_w1": nrm(ks[11], (DEPTH, CMP_BLOCK * A_HEAD_DIM, CMP_HIDDEN), (CMP_BLOCK * A_HEAD_DIM) ** -0.5),
        "cmp_v_w2": nrm(ks[12], (DEPTH, CMP_HIDDEN, A_HEAD_DIM), CMP_HIDDEN ** -0.5),
        "w_out": nrm(ks[13], (DEPTH, MIX_WIDTH, D_MODEL), MIX_WIDTH ** -0.5),
        "ln2_g": gain(ks[14], (DEPTH, D_MODEL)),
        "w_up": nrm(ks[15], (DEPTH, D_MODEL, 2 * D_FF), D_MODEL ** -0.5),
        "conv_w": nrm(ks[16], (DEPTH, CONV_WIDTH, D_FF), CONV_WIDTH ** -0.5),
        "conv_b": nrm(ks[17], (DEPTH, D_FF), 0.02),
        "w_down": nrm(ks[18], (DEPTH, D_FF, D_MODEL), D_FF ** -0.5),
        "ple_norm_g": gain(ks[19], (DEPTH, D_MODEL)),
        "w_ple_gate": nrm(ks[20], (DEPTH, D_MODEL, D_MODEL), D_MODEL ** -0.5),
        "w_ple_proj": nrm(ks[21], (DEPTH, PLE_DIM, D_MODEL), PLE_DIM ** -0.5),
        "final_g": gain(ks[22], (D_MODEL,)),
    }


def reference(x, p, ln1_g, w_in, mlstm_gate_bias, mlstm_norm_g, cmp_pos_k, cmp_pos_v, cmp_k_w1, cmp_k_w2,
              cmp_v_w1, cmp_v_w2, w_out, ln2_g, w_up, conv_w, conv_b, w_down, ple_norm_g, w_ple_gate,
              w_ple_proj, final_g):
    B, S, _ = x.shape
    sizes = [M_WIDTH] * 4 + [M_HEADS, M_HEADS, A_WIDTH] + [KV_WIDTH] * 6 + [3 * A_HEADS]
    offs = np.cumsum(sizes)[:-1].tolist()
    for i in range(DEPTH):
        h = rms_norm(x, ln1_g[i])
        z = h @ w_in[i]
        mq, mk, mv, mo, mi, mf, aq, akc, avc, aks, avs, akw, avw, ag = jnp.split(z, offs, axis=-1)
        mshape = (B, S, M_HEADS, M_HEAD_DIM)
        hm = mlstm(mq.reshape(mshape), mk.reshape(mshape), mv.reshape(mshape),
                   mi + mlstm_gate_bias[i, :M_HEADS], mf + mlstm_gate_bias[i, M_HEADS:])
        hm = rms_norm(hm, mlstm_norm_g[i].reshape(M_HEADS, M_HEAD_DIM)).reshape(B, S, M_WIDTH)
        m_out = (hm * jax.nn.sigmoid(mo.astype(jnp.float32))).astype(x.dtype)
        kvshape = (B, S, A_KV_GROUPS, A_HEAD_DIM)
        a_out = nsa(aq.reshape(B, S, A_HEADS, A_HEAD_DIM), akc.reshape(kvshape), avc.reshape(kvshape),
                    aks.reshape(kvshape), avs.reshape(kvshape), akw.reshape(kvshape), avw.reshape(kvshape),
                    ag, cmp_pos_k[i], cmp_pos_v[i], cmp_k_w1[i], cmp_k_w2[i], cmp_v_w1[i], cmp_v_w2[i]).astype(x.dtype)
        x = x + jnp.concatenate([m_out, a_out], axis=-1) @ w_out[i]
        x = x + conv_ffn(rms_norm(x, ln2_g[i]), w_up[i], conv_w[i], conv_b[i], w_down[i])
        gate = jax.nn.sigmoid(rms_norm(x, ple_norm_g[i]) @ w_ple_gate[i])
        x = x + gate * (p[i].astype(x.dtype) @ w_ple_proj[i])
    return rms_norm(x, final_g)
```

```python
import sys
import contextlib
import numpy as np
import concourse.bass as bass
import concourse.mybir as mybir
from concourse.bass_utils import run_bass_kernel_spmd

F32 = mybir.dt.float32
BF16 = mybir.dt.bfloat16
AF = mybir.ActivationFunctionType
ALU = mybir.AluOpType
AX = mybir.AxisListType

D = 1024
INW = 3360
NEGM = -30000.0
EPS = 1e-6
C_MQ, C_MK, C_MV, C_MO, C_MI, C_AQ, C_AKC, C_AVC, C_AKS, C_AVS, C_AKW, C_AVW, C_AG = (
    0, 512, 1024, 1536, 2048, 2056, 2568, 2696, 2824, 2952, 3080, 3208, 3336)


class _Rec:
    def __getattr__(self, name):
        def f(*a, **k):
            return (name, a, k)
        return f


R = _Rec()


class Prog:
    ENGS = ['pe', 'act', 'dve', 'pool', 'sp']

    def __init__(self, nc):
        self.nc = nc
        self.ops = []

    def add(self, eng, fn, reads=(), writes=(), dma_key=None):
        self.ops.append(dict(eng=eng, fn=fn, reads=list(reads), writes=list(writes),
                             dma_key=None if dma_key is None else str(dma_key),
                             deps=set(), sig=False, bar=False))

    def barrier(self):
        self.ops.append(dict(eng='dve', fn=None, reads=[], writes=[], dma_key=None,
                             deps=set(), sig=False, bar=True))

    @staticmethod
    def _norm(k):
        return k if isinstance(k, tuple) else (k, None)

    def build(self):
        nc = self.nc
        ops = self.ops
        state = {}

        def entries(name, sub):
            d = state.setdefault(name, {})
            if sub is None:
                return list(d.values())
            return [d[s] for s in (sub, None) if s in d]

        last_bar = None
        last_of = {}
        for i, op in enumerate(ops):
            if op['bar']:
                for k, j in last_of.items():
                    op['deps'].add((j, 'bar'))
                last_bar = i
                state = {}
                last_of = {}
                continue
            if last_bar is not None:
                op['deps'].add((last_bar, 'bar'))
            for k in op['reads']:
                name, sub = self._norm(k)
                for e in entries(name, sub):
                    if e[0] is not None:
                        op['deps'].add((e[0], 'raw'))
            for k in op['writes']:
                name, sub = self._norm(k)
                for e in entries(name, sub):
                    if e[0] is not None:
                        op['deps'].add((e[0], 'waw'))
                    for r in e[1]:
                        op['deps'].add((r, 'war'))
            for k in op['reads']:
                name, sub = self._norm(k)
                state.setdefault(name, {}).setdefault(sub, [None, []])[1].append(i)
            for k in op['writes']:
                name, sub = self._norm(k)
                d = state.setdefault(name, {})
                if sub is None:
                    d.clear()
                d[sub] = [i, []]
            last_of[op['dma_key'] if op['dma_key'] is not None else ('E', op['eng'])] = i
        for i, op in enumerate(ops):
            nd = set()
            for (j, kind) in op['deps']:
                if j == i:
                    continue
                oj = ops[j]
                if (oj['eng'] == op['eng'] and oj['dma_key'] is None and op['dma_key'] is None
                        and not oj['bar'] and not op['bar']):
                    if op['eng'] == 'pe':
                        continue
                nd.add(j)
            op['deps'] = nd
            for j in nd:
                if ops[j]['dma_key'] is None:
                    ops[j]['sig'] = True
        st = contextlib.ExitStack()
        sems = {e: st.enter_context(nc.semaphore('s_' + e)) for e in self.ENGS}
        keys = sorted({op['dma_key'] for op in ops if op['dma_key'] is not None})
        dsems = {k: st.enter_context(nc.semaphore('d%d' % n)) for n, k in enumerate(keys)}
        cnt = {e: 0 for e in self.ENGS}
        dcnt = {k: 0 for k in keys}
        for op in ops:
            if op['dma_key'] is not None:
                dcnt[op['dma_key']] += 16
                op['done'] = (dsems[op['dma_key']], dcnt[op['dma_key']])
            elif op['sig']:
                cnt[op['eng']] += 1
                op['done'] = (sems[op['eng']], cnt[op['eng']])
        final = [(dsems[k], dcnt[k]) for k in keys]
        waited = {e: {} for e in self.ENGS}
        for op in ops:
            w = {}
            for j in op['deps']:
                s, v = ops[j]['done']
                if v > w.get(id(s), (None, 0))[1]:
                    w[id(s)] = (s, v)
            op['waits'] = []
            for key, (s, v) in w.items():
                if waited[op['eng']].get(key, 0) >= v:
                    continue
                waited[op['eng']][key] = v
                op['waits'].append((s, v))
        print('prog: ops=%d sig=%s ndma_keys=%d waits=%d' % (
            len(ops), cnt, len(keys), sum(len(o['waits']) for o in ops)), file=sys.stderr)

        def emit(engname, e):
            for op in ops:
                if op['eng'] != engname:
                    continue
                for (s, v) in op['waits']:
                    e.wait_ge(s, v)
                if op['bar']:
                    ins = e.memset(self.bar_tile[:], 0.0)
                else:
                    name, a, k = op['fn']
                    ins = getattr(e, name)(*a, **k)
                if op['dma_key'] is not None:
                    ins.then_inc(op['done'][0], 16)
                elif op['sig']:
                    ins.then_inc(op['done'][0], 1)
            if engname == 'sp':
                for (s, v) in final:
                    e.wait_ge(s, v)

        with nc.Block() as block:
            @block.tensor
            def _(e):
                emit('pe', e)

            @block.scalar
            def _(e):
                emit('act', e)

            @block.vector
            def _(e):
                emit('dve', e)

            @block.gpsimd
            def _(e):
                emit('pool', e)

            @block.sync
            def _(e):
                emit('sp', e)
        st.close()


def host_consts(S):
    NT = S // 128
    c = {}
    c['c_ident'] = np.eye(128, dtype=np.float32)
    sl = np.arange(128)[:, None]
    tl = np.arange(128)[None, :]
    c['c_tri'] = (sl <= tl).astype(np.float32)
    c['c_ones'] = np.ones((128, 128), np.float32)
    masks = [np.where(sl <= tl, 0.0, NEGM), np.where(sl > tl, 0.0, NEGM)]
    for k in range(17):
        masks.append(np.where(128 * k + tl - 16 * sl - 31 >= 0, 0.0, NEGM))
    c['c_mask'] = np.ascontiguousarray(np.stack(masks, 1).astype(np.float32))
    E = np.zeros((64, NT, 128), np.float32)
    for kt in range(NT):
        for s in range(128):
            E[2 * kt + s // 64, kt, s] = 1.0
    c['c_E'] = E
    s = np.arange(S)
    c['c_kaug'] = np.stack([np.ones(S), np.ones(S), 64.0 * (s // 64), s % 64]).astype(np.float32)
    n = np.arange(256)
    pos = 16 * n + 31
    c['c_kcaug'] = np.stack([np.ones(256), np.ones(256), 64.0 * (pos // 64), pos % 64]).astype(np.float32)
    slope = 2.0 ** (-(np.arange(8) + 1.0))
    qa = np.zeros((4, 8, S), np.float32)
    for h in range(8):
        qa[0, h] = -slope[h] * 64.0 * (s // 64)
        qa[1, h] = -slope[h] * (s % 64)
        qa[2, h] = slope[h]
        qa[3, h] = slope[h]
    c['c_qaug'] = qa
    ov = np.zeros((256, 65), np.float32)
    for nn in range(255):
        for j in range(64):
            if 16 * nn <= 64 * j + 63 and 16 * nn + 31 >= 64 * j:
                ov[nn, j] = 1.0
        ov[nn, 64] = 1.0
    c['c_ovl'] = ov
    t = np.arange(S)[:, None]
    j = np.arange(64)[None, :]
    cur = t // 64
    forced = (j == 0) | (j == cur) | (j == cur - 1)
    causal = (64 * j <= t)
    selmul = (causal & ~forced).astype(np.float32)
    seladd = np.where(causal, np.where(forced, 8.0 + 0.01 * j, 0.0), -(1.0 + j)).astype(np.float32)
    c['c_selmul'] = np.ascontiguousarray(selmul.reshape(NT, 128, 64).transpose(1, 0, 2))
    c['c_seladd'] = np.ascontiguousarray(seladd.reshape(NT, 128, 64).transpose(1, 0, 2))
    return c


CONST_SHAPES = lambda S: {k: v.shape for k, v in host_consts(S).items()}


def host_weights(ins):
    f = lambda a: np.ascontiguousarray(np.asarray(a, dtype=np.float32))
    w = {}
    w['w_in'] = f(ins['w_in'][0])
    w['ln1_g'] = f(ins['ln1_g'][0].reshape(8, 128).T)
    w['gbias'] = f(ins['mlstm_gate_bias'][0].reshape(1, 8))
    w['mng'] = f(ins['mlstm_norm_g'][0].reshape(1, 512))
    w['pek'] = f(np.tile(ins['cmp_pos_k'][0].T, (2, 1)))
    w['pev'] = f(np.tile(ins['cmp_pos_v'][0].T, (2, 1)))
    r1 = lambda a: np.tile(a.reshape(32, 64, 128).transpose(1, 0, 2), (2, 1, 1))
    w['w1k'] = f(r1(ins['cmp_k_w1'][0]))
    w['w1v'] = f(r1(ins['cmp_v_w1'][0]))
    w['w2k'] = f(ins['cmp_k_w2'][0])
    w['w2v'] = f(ins['cmp_v_w2'][0])
    w['w_out'] = f(ins['w_out'][0])
    w['ln2_g'] = f(ins['ln2_g'][0].reshape(8, 128).T)
    w['w_up'] = f(ins['w_up'][0])
    w['conv_w'] = f(ins['conv_w'][0].reshape(3, 16, 128).transpose(2, 1, 0))
    w['conv_b'] = f(ins['conv_b'][0].reshape(16, 128).T)
    w['w_down'] = f(ins['w_down'][0])
    w['pln_g'] = f(ins['ple_norm_g'][0].reshape(8, 128).T)
    w['w_pg'] = f(ins['w_ple_gate'][0])
    w['w_pp'] = f(ins['w_ple_proj'][0])
    w['final_g'] = f(ins['final_g'].reshape(1, 1024))
    return w


def build_program(S, NB, phases='AB', dbg=False):
    NT = S // 128
    nc = bass.Bass("TRN2", target_bir_lowering=False)
    dr = {}

    def din(name, shape):
        dr[name] = nc.dram_tensor(name, list(shape), F32, kind="ExternalInput").ap()
        return dr[name]

    x = din('x', [NB, S, D])
    pin = din('p', [NB, S, 256])
    for k, shp in CONST_SHAPES(S).items():
        din(k, shp)
    wshapes = dict(w_in=[D, INW], ln1_g=[128, 8], gbias=[1, 8], mng=[1, 512], pek=[128, 32], pev=[128, 32],
                   w1k=[128, 32, 128], w1v=[128, 32, 128], w2k=[128, 64], w2v=[128, 64], w_out=[D, D],
                   ln2_g=[128, 8], w_up=[D, 4096], conv_w=[128, 16, 3], conv_b=[128, 16], w_down=[2048, D],
                   pln_g=[128, 8], w_pg=[D, D], w_pp=[256, D], final_g=[1, D])
    for k, shp in wshapes.items():
        din(k, shp)
    y = nc.dram_tensor('y', [NB, S, D], F32, kind="ExternalOutput").ap()
    x1d = nc.dram_tensor('x1scr', [NB, S, D], F32, kind="Internal").ap()

    P = Prog(nc)
    top = contextlib.ExitStack()
    P.bar_tile = top.enter_context(nc.sbuf_tensor('bar_tile', [128, 1], F32))

    def mk(stack):
        def sb(name, shape, dt=F32):
            return stack.enter_context(nc.sbuf_tensor(name, list(shape), dt))

        def ps(name, shape, dt=F32):
            return stack.enter_context(nc.psum_tensor(name, list(shape), dt))
        return sb, ps

    def load(q, tile_ap, src_ap, key):
        P.add('sp', R.dma_start(out=tile_ap, in_=src_ap), writes=[key], dma_key=key)

    stg = [top.enter_context(nc.sbuf_tensor('stg%d' % j, [128, 512], F32)) for j in range(2)]
    cast_ctr = [0]

    def load_cast(dst2d, src2d, key, p0=0):
        pp, n = src2d.shape[0], src2d.shape[1]
        for c0 in range(0, n, 512):
            w = min(512, n - c0)
            j = cast_ctr[0] % 2
            cast_ctr[0] += 1
            P.add('sp', R.dma_start(out=stg[j][p0:p0 + pp, 0:w], in_=src2d[:, c0:c0 + w]),
                  writes=['stg%d' % j], dma_key='stg%d' % j)
            if j == 0:
                P.add('dve', R.tensor_copy(out=dst2d[:, c0:c0 + w], in_=stg[j][p0:p0 + pp, 0:w]),
                      reads=['stg%d' % j], writes=[key])
            else:
                P.add('act', R.activation(out=dst2d[:, c0:c0 + w], in_=stg[j][p0:p0 + pp, 0:w], func=AF.Copy),
                      reads=['stg%d' % j], writes=[key])

    if 'A' in phases:
        A = contextlib.ExitStack()
        sb, ps = mk(A)
        w_in = sb('w_in_sb', [128, 8, INW], BF16)
        w_out = sb('w_out_sb', [128, 8, D], BF16)
        w1 = [sb('w1k_sb', [128, 32, 128], BF16), sb('w1v_sb', [128, 32, 128], BF16)]
        w2k = sb('w2k_sb', [128, 64], BF16)
        w2v = sb('w2v_sb', [128, 64], BF16)
        pe_t = [sb('pek_sb', [128, 32], BF16), sb('pev_sb', [128, 32], BF16)]
        g1 = sb('g1_sb', [128, 8])
        gbias = sb('gbias_sb', [128, 8])
        mng = sb('mng_sb', [128, 512])
        identb = sb('identb', [128, 128], BF16)
        tri = sb('tri', [128, 128])
        ones = sb('ones', [128, 128])
        cmask = sb('cmask', [128, 19, 128], BF16)
        Esel = sb('Esel', [64, NT, 128], BF16)
        ovl = None
        selmul = sb('selmul', [128, 64])
        seladd = sb('seladd', [128, 64])
        ccmp = [sb('cck', [128, 1]), sb('ccv', [128, 1])]
        KsT = sb('KsT', [68, 2, S], BF16)
        KwT = sb('KwT', [68, 2, 640], BF16)
        KcT = sb('KcT', [68, 2, 256], BF16)
        Vs = sb('Vs', [128, NT, 2, 65], BF16)
        Vw = sb('Vw', [128, 5, 2, 65], BF16)
        VcX = sb('VcX', [128, 2, 2, 129], BF16)
        Hv = sb('Hv', [128, 2, 256], BF16)
        rawT = [sb('kcrawT', [128, 144], BF16), sb('vcrawT', [128, 144], BF16)]
        Cst = sb('Cst', [128, 4, 129])
        Cb = sb('Cb', [128, 4, 129], BF16)
        xt = sb('xt', [128, D])
        ssq = sb('ssqA', [128, 1])
        rstd = sb('rstdA', [128, 1])
        hb = sb('hb', [128, D], BF16)
        hT = sb('hT', [128, 8, 128], BF16)
        ktok = sb('ktok', [128, 4, 128], BF16)
        ku = sb('ku', [128, 4, 128], BF16)
        vtok = sb('vtok', [128, 4, 129], BF16)
        smo = sb('smo', [128, 512])
        gates = sb('gates', [128, 8])
        gsig = sb('gsig', [128, 24])
        qT = sb('qT', [128, 4, 128], BF16)
        kT = sb('kT', [128, 4, 128], BF16)
        QT = sb('QT', [68, 8, 128], BF16)
        lf = sb('lf', [128, 4])
        e1 = sb('e1', [128, 4])
        bcs = sb('bcs', [128, 8])
        uu = sb('uu', [128, 4])
        eb = sb('eb', [128, 4])
        ebL = sb('ebL', [128, 4])
        AT = sb('AT', [128, 128], BF16)
        h4 = sb('h4', [128, 4, 128])
        den = sb('den', [128, 4])
        fac = sb('fac', [128, 4])
        sq4 = sb('sq4', [128, 4, 128])
        ssq4 = sb('ssq4', [128, 4])
        rstd4 = sb('rstd4', [128, 4])
        mix = sb('mix', [128, D])
        mixb = sb('mixb', [128, D], BF16)
        mixT = sb('mixT', [128, 8, 128], BF16)
        hx = sb('hx', [128, 8])
        hx2 = sb('hx2', [128, 8])
        hact = sb('hact', [128, 8], BF16)
        PT = sb('PT', [128, 4, 128], BF16)
        OcS = sb('OcS', [128, 4, 129])
        rsc = sb('rsc', [128, 4])
        imp = sb('imp', [128, 64])
        score = sb('score', [128, 64])
        sc2 = sb('sc2', [128, 64])
        m8a = sb('m8a', [128, 8])
        m8b = sb('m8b', [128, 8])
        MB = sb('MB', [128, 64])
        MBb = sb('MBb', [128, 64], BF16)
        MBT = sb('MBT', [64, 128], BF16)
        sums = sb('sums', [128, 8, 3])
        coef = sb('coef', [128, 8, 3])
        OsS = sb('OsS', [128, 8, 64])
        x1t = sb('x1t', [128, D])
        zt = sb('zt', [128, 260], BF16)
        pT = ps('pT', [128, 8, 128], BF16)
        pz = ps('pz', [128, 512])
        pml = ps('pml', [128, 512])
        pm2 = ps('pm2', [128, 512])
        pm3 = ps('pm3', [128, 512])
        pst = ps('pst', [128, 4, 128])
        po5 = ps('po5', [128, 512])
        po6 = ps('po6', [128, 512])

        for kc in range(8):
            load_cast(w_in[:, kc, :], dr['w_in'][kc * 128:(kc + 1) * 128, :], 'w_in')
            load_cast(w_out[:, kc, :], dr['w_out'][kc * 128:(kc + 1) * 128, :], 'w_out')
        load_cast(w1[0][:].rearrange("p l h -> p (l h)"), dr['w1k'].rearrange("p l h -> p (l h)"), 'w1k')
        load_cast(w1[1][:].rearrange("p l h -> p (l h)"), dr['w1v'].rearrange("p l h -> p (l h)"), 'w1v')
        load_cast(w2k[:], dr['w2k'], 'w2k')
        load_cast(w2v[:], dr['w2v'], 'w2v')
        load_cast(pe_t[0][:], dr['pek'], 'pek')
        load_cast(pe_t[1][:], dr['pev'], 'pev')
        load('sp', g1[:], dr['ln1_g'], 'g1')
        load('sp', gbias[:], dr['gbias'].partition_broadcast(128), 'gbias')
        load('sp', mng[:], dr['mng'].partition_broadcast(128), 'mng')
        load_cast(identb[:], dr['c_ident'], 'identb')
        load('sp', tri[:], dr['c_tri'], 'tri')
        load('sp', ones[:], dr['c_ones'], 'ones')
        load_cast(cmask[:].rearrange("p m t -> p (m t)"), dr['c_mask'].rearrange("p m t -> p (m t)"), 'cmask')
        load_cast(Esel[:].rearrange("p k t -> p (k t)"), dr['c_E'].rearrange("p k t -> p (k t)"), 'Esel')
        P.add('dve', R.memset(zt[:], 0.0), writes=['zt'])
        for nm, t_, val in (('KsT', KsT, 0.0), ('KwT', KwT, 0.0), ('KcT', KcT, 0.0), ('Vs', Vs, 1.0),
                            ('Vw', Vw, 1.0), ('VcX', VcX, 0.0), ('Hv', Hv, 0.0), ('vtok', vtok, 1.0)):
            P.add('dve', R.memset(t_[:], val), writes=[nm])
        for g in range(2):
            load_cast(KsT[64:68, g, :], dr['c_kaug'], 'KsT', p0=64)
            load_cast(KcT[64:68, g, :], dr['c_kcaug'], 'KcT', p0=64)
            for c in range(2):
                load_cast(VcX[:, c, g, 64:129], dr['c_ovl'][c * 128:(c + 1) * 128, :], 'VcX')
        for kv in range(2):
            for l in range(32):
                P.add('pe', R.matmul(pml[:, 0:1], lhsT=w1[kv][0:64, l, :],
                                                           rhs=pe_t[kv][0:64, l:l + 1],
                                                           start=(l == 0), stop=(l == 31)),
                      reads=['w1k', 'w1v', 'pek', 'pev'], writes=['pml'])
            P.add('dve', R.tensor_copy(out=ccmp[kv][:], in_=pml[:, 0:1]),
                  reads=['pml'], writes=['cc%d' % kv])

        for b in range(NB):
            P.add('dve', R.memset(Cst[:], 0.0), writes=['Cst'])
            P.add('dve', R.memset(Cb[:], 0.0), writes=['Cb'])
            for i in range(NT):
                t0 = 128 * i
                load('sp', selmul[:], dr['c_selmul'][:, i, :], 'selmul')
                load('sp', seladd[:], dr['c_seladd'][:, i, :], 'seladd')
                load('sp', xt[:], x[b, t0:t0 + 128, :], 'xt')
                P.add('act', R.activation(out=x1t[:], in_=xt[:], func=AF.Square, accum_out=ssq[:]),
                      reads=['xt'], writes=['x1t', 'ssqA'])
                P.add('act', R.activation(out=rstd[:], in_=ssq[:], func=AF.Ln, bias=EPS, scale=1.0 / D),
                      reads=['ssqA'], writes=['rstdA'])
                P.add('act', R.activation(out=rstd[:], in_=rstd[:], func=AF.Exp, scale=-0.5),
                      reads=['rstdA'], writes=['rstdA'])
                P.add('dve', R.tensor_scalar(out=hb[:], in0=xt[:], scalar1=rstd[:, 0:1], scalar2=None,
                                                       op0=ALU.mult),
                      reads=['xt', 'rstdA'], writes=['hb'])
                for kc in range(8):
                    P.add('pe', R.transpose(out=pT[:, kc, :], in_=hb[:, kc * 128:(kc + 1) * 128],
                                                             identity=identb[:]),
                          reads=['hb', 'identb'], writes=['pT'])
                for kc in range(8):
                    P.add('dve', R.tensor_scalar(out=hT[:, kc, :], in0=pT[:, kc, :],
                                                                  scalar1=g1[:, kc:kc + 1], scalar2=None,
                                                                  op0=ALU.mult),
                          reads=['pT', 'g1'], writes=['hT'])

                def tok_proj(col0, ncol, out_ap):
                    for kc in range(8):
                        P.add('pe', R.matmul(out_ap, lhsT=hT[:, kc, :],
                                                              rhs=w_in[:, kc, col0:col0 + ncol],
                                                              start=(kc == 0), stop=(kc == 7)),
                              reads=['hT', 'w_in'], writes=['pz'])

                def feat_proj(col0, m, out_ap, wr):
                    for kc in range(8):
                        P.add('pe', R.matmul(out_ap, lhsT=w_in[:, kc, col0:col0 + m],
                                                              rhs=hT[:, kc, :],
                                                              start=(kc == 0), stop=(kc == 7)),
                              reads=['hT', 'w_in'], writes=[wr])

                tok_proj(C_MK, 512, pz[:, 0:512])
                P.add('act', R.activation(out=ktok[:], in_=pz[:, 0:512].rearrange("p (h d) -> p h d", h=4),
                                                    func=AF.Copy),
                      reads=['pz'], writes=['ktok'])
                tok_proj(C_MV, 512, pz[:, 0:512])
                P.add('act', R.activation(out=vtok[:, :, 0:128],
                                                    in_=pz[:, 0:512].rearrange("p (h d) -> p h d", h=4),
                                                    func=AF.Copy),
                      reads=['pz'], writes=['vtok'])
                tok_proj(C_MO, 512, pz[:, 0:512])
                P.add('act', R.activation(out=smo[:], in_=pz[:, 0:512], func=AF.Sigmoid),
                      reads=['pz'], writes=['smo'])
                tok_proj(C_MI, 8, pz[:, 0:8])
                tok_proj(C_AVS, 128, pz[:, 8:136])
                tok_proj(C_AVW, 128, pz[:, 136:264])
                tok_proj(C_AG, 24, pz[:, 264:288])
                P.add('dve', R.tensor_tensor(out=gates[:], in0=pz[:, 0:8], in1=gbias[:], op=ALU.add),
                      reads=['pz', 'gbias'], writes=['gates'])
                P.add('dve', R.tensor_copy(out=Vs[:, i, :, 0:64],
                                                          in_=pz[:, 8:136].rearrange("p (g d) -> p g d", g=2)),
                      reads=['pz'], writes=[('Vs', i)])
                P.add('dve', R.tensor_copy(out=Vw[:, i % 5, :, 0:64],
                                                          in_=pz[:, 136:264].rearrange("p (g d) -> p g d", g=2)),
                      reads=['pz'], writes=[('Vw', i % 5)])
                P.add('act', R.activation(out=gsig[:], in_=pz[:, 264:288], func=AF.Sigmoid),
                      reads=['pz'], writes=['gsig'])
                for h in range(4):
                    feat_proj(C_MQ + 128 * h, 128, pz[:, 128 * h:128 * h + 128], 'pz')
                P.add('act', R.activation(out=qT[:], in_=pz[:, 0:512].rearrange("p (h t) -> p h t", h=4),
                                                    func=AF.Copy, scale=float(128.0 ** -0.5)),
                      reads=['pz'], writes=['qT'])
                for h in range(4):
                    feat_proj(C_MK + 128 * h, 128, pz[:, 128 * h:128 * h + 128], 'pz')
                P.add('act', R.activation(out=kT[:], in_=pz[:, 0:512].rearrange("p (h t) -> p h t", h=4),
                                                    func=AF.Copy),
                      reads=['pz'], writes=['kT'])
                for hh in range(2):
                    for h in range(4):
                        feat_proj(C_AQ + 64 * (4 * hh + h), 64, pz[0:64, 128 * h:128 * h + 128], 'pz')
                    P.add('act', R.activation(
                        out=QT[0:64, 4 * hh:4 * hh + 4, :], in_=pz[0:64, 0:512].rearrange("p (h t) -> p h t", h=4),
                        func=AF.Copy, scale=0.125), reads=['pz'], writes=['QT'])
                for hh in range(2):
                    jq = cast_ctr[0] % 2
                    cast_ctr[0] += 1
                    qv = stg[jq][64:68, 0:512].rearrange("p (h t) -> p h t", h=4)
                    P.add('sp', R.dma_start(out=qv, in_=dr['c_qaug'][:, 4 * hh:4 * hh + 4, t0:t0 + 128]),
                          writes=['stg%d' % jq], dma_key='stg%d' % jq)
                    P.add('act', R.activation(out=QT[64:68, 4 * hh:4 * hh + 4, :], in_=qv, func=AF.Copy),
                          reads=['stg%d' % jq], writes=['QT'])
                jq = cast_ctr[0] % 2
                cast_ctr[0] += 1
                P.add('sp', R.dma_start(out=stg[jq][64:68, 0:128], in_=dr['c_kaug'][:, t0:t0 + 128]),
                      writes=['stg%d' % jq], dma_key='stg%d' % jq)
                P.add('act', R.activation(out=KwT[64:68, :, 128 * (i % 5):128 * (i % 5) + 128],
                                          in_=stg[jq][64:68, 0:128].unsqueeze(1).to_broadcast([4, 2, 128]),
                                          func=AF.Copy), reads=['stg%d' % jq], writes=[('KwT', i % 5)])
                feat_proj(C_AKC, 128, pz[:, 0:128], 'pz')
                feat_proj(C_AVC, 128, pz[:, 128:256], 'pz')
                for g in range(2):
                    feat_proj(C_AKS + 64 * g, 64, pz[0:64, 256 + 128 * g:384 + 128 * g], 'pz')
                P.add('act', R.activation(out=rawT[0][:, 16:144], in_=pz[:, 0:128], func=AF.Copy),
                      reads=['pz'], writes=['rawk'])
                P.add('act', R.activation(out=rawT[1][:, 16:144], in_=pz[:, 128:256], func=AF.Copy),
                      reads=['pz'], writes=['rawv'])
                P.add('act', R.activation(
                    out=KsT[0:64, :, t0:t0 + 128], in_=pz[0:64, 256:512].rearrange("p (g t) -> p g t", g=2),
                    func=AF.Copy), reads=['pz'], writes=[('KsT', i)])
                for g in range(2):
                    feat_proj(C_AKW + 64 * g, 64, pz[0:64, 128 * g:128 + 128 * g], 'pz')
                P.add('act', R.activation(
                    out=KwT[0:64, :, 128 * (i % 5):128 * (i % 5) + 128],
                    in_=pz[0:64, 0:256].rearrange("p (g t) -> p g t", g=2),
                    func=AF.Copy), reads=['pz'], writes=[('KwT', i % 5)])

                P.add('act', R.activation(out=e1[:], in_=gates[:, 4:8], func=AF.Exp, scale=-1.0),
                      reads=['gates'], writes=['e1'])
                P.add('act', R.activation(out=lf[:], in_=e1[:], func=AF.Ln, bias=1.0, scale=1.0),
                      reads=['e1'], writes=['lf'])
                P.add('pe', R.matmul(pml[:, 0:4], lhsT=tri[:], rhs=lf[:], start=True, stop=True),
                      reads=['tri', 'lf'], writes=['pml'])
                P.add('pe', R.matmul(pml[:, 4:8], lhsT=ones[:], rhs=lf[:], start=True, stop=True),
                      reads=['ones', 'lf'], writes=['pml'])
                P.add('dve', R.tensor_copy(out=bcs[:], in_=pml[:, 0:8]), reads=['pml'], writes=['bcs'])
                P.add('dve', R.tensor_tensor(out=uu[:], in0=gates[:, 0:4], in1=bcs[:, 0:4], op=ALU.add),
                      reads=['gates', 'bcs'], writes=['uu'])
                P.add('act', R.activation(out=uu[:], in_=uu[:], func=AF.Exp), reads=['uu'], writes=['uu'])
                P.add('act', R.activation(out=eb[:], in_=bcs[:, 0:4], func=AF.Exp, scale=-1.0),
                      reads=['bcs'], writes=['eb'])
                P.add('act', R.activation(out=ebL[:], in_=bcs[:, 4:8], func=AF.Exp, scale=-1.0),
                      reads=['bcs'], writes=['ebL'])
                for h in range(4):
                    P.add('pe', R.matmul(pml[:, 0:128], lhsT=kT[:, h, :], rhs=qT[:, h, :],
                                                        start=True, stop=True),
                          reads=['kT', 'qT'], writes=['pml'])
                    P.add('dve', R.scalar_tensor_tensor(out=AT[:], in0=pml[:, 0:128],
                                                                       scalar=uu[:, h:h + 1], in1=tri[:],
                                                                       op0=ALU.mult, op1=ALU.mult),
                          reads=['pml', 'uu', 'tri'], writes=['AT'])
                    P.add('pe', R.matmul(pm2[:, 0:129], lhsT=AT[:], rhs=vtok[:, h, :],
                                                        start=True, stop=False),
                          reads=['AT', 'vtok'], writes=['pm2'])
                    P.add('pe', R.matmul(pm2[:, 0:129], lhsT=qT[:, h, :], rhs=Cb[:, h, :],
                                                        start=False, stop=True),
                          reads=['qT', 'Cb'], writes=['pm2'])
                    P.add('dve', R.tensor_tensor(out=den[:, h:h + 1], in0=pm2[:, 128:129],
                                                                in1=eb[:, h:h + 1], op=ALU.mult),
                          reads=['pm2', 'eb'], writes=['den'])
                    P.add('dve', R.scalar_tensor_tensor(out=den[:, h:h + 1], in0=den[:, h:h + 1],
                                                                       scalar=-1.0, in1=den[:, h:h + 1],
                                                                       op0=ALU.mult, op1=ALU.max),
                          reads=['den'], writes=['den'])
                    P.add('dve', R.tensor_scalar(out=den[:, h:h + 1], in0=den[:, h:h + 1],
                                                                scalar1=1.0, scalar2=None, op0=ALU.max),
                          reads=['den'], writes=['den'])
                    P.add('dve', R.reciprocal(out=fac[:, h:h + 1], in_=den[:, h:h + 1]),
                          reads=['den'], writes=['fac'])
                    P.add('dve', R.tensor_tensor(out=fac[:, h:h + 1], in0=fac[:, h:h + 1],
                                                                in1=eb[:, h:h + 1], op=ALU.mult),
                          reads=['fac', 'eb'], writes=['fac'])
                    P.add('dve', R.tensor_scalar(out=h4[:, h, :], in0=pm2[:, 0:128],
                                                                scalar1=fac[:, h:h + 1], scalar2=None, op0=ALU.mult),
                          reads=['pm2', 'fac'], writes=['h4'])
                    P.add('dve', R.tensor_scalar(out=ku[:, h, :], in0=ktok[:, h, :],
                                                                scalar1=uu[:, h:h + 1], scalar2=None, op0=ALU.mult),
                          reads=['ktok', 'uu'], writes=['ku'])
                    P.add('pe', R.matmul(pm3[:, 0:129], lhsT=ku[:, h, :], rhs=vtok[:, h, :],
                                                        start=True, stop=True),
                          reads=['ku', 'vtok'], writes=['pm3'])
                    P.add('dve', R.tensor_tensor(out=Cst[:, h, :], in0=Cst[:, h, :],
                                                                in1=pm3[:, 0:129], op=ALU.add),
                          reads=['Cst', 'pm3', 'Cb'], writes=['Cst'])
                    P.add('dve', R.tensor_scalar(out=Cst[:, h, :], in0=Cst[:, h, :],
                                                                scalar1=ebL[:, h:h + 1], scalar2=None, op0=ALU.mult),
                          reads=['Cst', 'ebL'], writes=['Cst'])
                    P.add('act', R.activation(out=Cb[:, h, :], in_=Cst[:, h, :], func=AF.Copy),
                          reads=['Cst'], writes=['Cb'])
                P.add('dve', R.tensor_tensor(out=sq4[:], in0=h4[:], in1=h4[:], op=ALU.mult),
                      reads=['h4'], writes=['sq4'])
                P.add('dve', R.reduce_sum(out=ssq4[:], in_=sq4[:], axis=AX.X), reads=['sq4'], writes=['ssq4'])
                P.add('act', R.activation(out=rstd4[:], in_=ssq4[:], func=AF.Ln, bias=EPS, scale=1.0 / 128),
                      reads=['ssq4'], writes=['rstd4'])
                P.add('act', R.activation(out=rstd4[:], in_=rstd4[:], func=AF.Exp, scale=-0.5),
                      reads=['rstd4'], writes=['rstd4'])
                P.add('dve', R.tensor_tensor(out=smo[:], in0=smo[:], in1=mng[:], op=ALU.mult),
                      reads=['smo', 'mng'], writes=['smo'])
                for h in range(4):
                    P.add('dve', R.scalar_tensor_tensor(
                        out=mix[:, 128 * h:128 * h + 128], in0=h4[:, h, :], scalar=rstd4[:, h:h + 1],
                        in1=smo[:, 128 * h:128 * h + 128], op0=ALU.mult, op1=ALU.mult),
                        reads=['h4', 'rstd4', 'smo'], writes=['mix'])

                n0 = max(0, 8 * i - 1)
                n1 = 8 * i + 6
                nn = n1 - n0 + 1
                for kv in range(2):
                    for g in range(2):
                        for l in range(32):
                            tok = 16 * n0 + l - t0 + 16
                            P.add('pe', R.matmul(
                                pm3[:, 386:386 + nn], lhsT=w1[kv][64 * g:64 * g + 64, l, :],
                                rhs=rawT[kv][64 * g:64 * g + 64, tok:tok + 16 * (nn - 1) + 1:16],
                                start=(l == 0), stop=(l == 31)),
                                reads=['rawk', 'rawv', 'w1k', 'w1v'], writes=['pm3'])
                        P.add('dve', R.tensor_scalar(out=hx[:, 0:nn], in0=pm3[:, 386:386 + nn],
                                                                      scalar1=ccmp[kv][:, 0:1], scalar2=None,
                                                                      op0=ALU.add),
                              reads=['pm3', 'cc%d' % kv], writes=['hx'])
                        P.add('dve', R.tensor_tensor(out=hx2[:, 0:nn], in0=hx[:, 0:nn], in1=hx[:, 0:nn],
                                                               op=ALU.mult), reads=['hx'], writes=['hx2'])
                        P.add('dve', R.tensor_scalar(out=hx2[:, 0:nn], in0=hx2[:, 0:nn], scalar1=0.044715,
                                                               scalar2=1.0, op0=ALU.mult, op1=ALU.add),
                              reads=['hx2'], writes=['hx2'])
                        P.add('dve', R.tensor_tensor(out=hx2[:, 0:nn], in0=hx2[:, 0:nn], in1=hx[:, 0:nn],
                                                               op=ALU.mult), reads=['hx2', 'hx'], writes=['hx2'])
                        P.add('act', R.activation(out=hx2[:, 0:nn], in_=hx2[:, 0:nn], func=AF.Sigmoid,
                                                            scale=1.5957691216057308),
                              reads=['hx2'], writes=['hx2'])
                        if kv == 0:
                            P.add('dve', R.tensor_tensor(out=hact[:, 0:nn], in0=hx2[:, 0:nn],
                                                                   in1=hx[:, 0:nn], op=ALU.mult),
                                  reads=['hx2', 'hx'], writes=['hact'])
                            P.add('pe', R.matmul(pm3[0:64, 400:400 + nn], lhsT=w2k[:], rhs=hact[:, 0:nn],
                                                           start=True, stop=True),
                                  reads=['hact', 'w2k'], writes=['pm3'])
                            P.add('dve', R.tensor_copy(out=KcT[0:64, g, n0:n0 + nn],
                                                                      in_=pm3[0:64, 400:400 + nn]),
                                  reads=['pm3'], writes=['KcT'])
                        else:
                            P.add('dve', R.tensor_tensor(out=Hv[:, g, n0:n0 + nn], in0=hx2[:, 0:nn],
                                                                        in1=hx[:, 0:nn], op=ALU.mult),
                                  reads=['hx2', 'hx'], writes=['Hv'])
                for kv in range(2):
                    P.add('act', R.activation(out=rawT[kv][:, 0:16], in_=rawT[kv][:, 128:144], func=AF.Copy),
                          reads=['rawk' if kv == 0 else 'rawv'], writes=['rawk' if kv == 0 else 'rawv'])
                for c in sorted({n0 // 128, n1 // 128}):
                    for g in range(2):
                        P.add('pe', R.matmul(pm3[:, 416:480], lhsT=Hv[:, g, 128 * c:128 * c + 128],
                                                                 rhs=w2v[:], start=True, stop=True),
                              reads=['Hv', 'w2v'], writes=['pm3'])
                        P.add('dve', R.tensor_copy(out=VcX[:, c, g, 0:64], in_=pm3[:, 416:480]),
                              reads=['pm3'], writes=['VcX'])

                def score_tile(kcache, kname, col0, g, masks):
                    P.add('pe', R.matmul(pst[:], lhsT=kcache[0:68, g, col0:col0 + 128],
                                                   rhs=QT[0:68, 4 * g:4 * g + 4, :],
                                                   start=True, stop=(len(masks) == 0)),
                          reads=[kname, 'QT'], writes=['pst'])
                    for mi, (lh, rh, rd) in enumerate(masks):
                        kk = rh.shape[0]
                        P.add('pe', R.matmul(
                            pst[0:128, :, :], lhsT=lh, rhs=rh.unsqueeze(1).to_broadcast([kk, 4, 128]), start=False,
                            stop=(mi == len(masks) - 1)), reads=rd, writes=['pst'])
                    P.add('act', R.activation(out=PT[:], in_=pst[:], func=AF.Exp),
                          reads=['pst'], writes=['PT'])

                Oc = [po5[:, 0:258].rearrange("p (r c) -> p r c", r=2), po6[:, 0:258].rearrange("p (r c) -> p r c", r=2)]
                Os = po5[:, 0:260].rearrange("p (r c) -> p r c", r=4)
                Ow = po6[:, 0:260].rearrange("p (r c) -> p r c", r=4)
                for g in range(2):
                    chunks = [c for c in range(2) if i - 16 * c >= 0]
                    for ci, c in enumerate(chunks):
                        k = i - 16 * c
                        masks = []
                        if k < 17:
                            masks.append((identb[:], cmask[:, 2 + k, :], ['identb', 'cmask']))
                        score_tile(KcT, 'KcT', 128 * c, g, masks)
                        if ci == 0:
                            P.add('pe', R.matmul(po5[:, 0:258], lhsT=zt[:, 0:128], rhs=zt[:, 0:258], start=True, stop=False),
                                  reads=['zt'], writes=['po5'])
                            P.add('pe', R.matmul(po6[:, 0:258], lhsT=zt[:, 0:128], rhs=zt[:, 0:258], start=True, stop=False),
                                  reads=['zt'], writes=['po6'])
                        for r in range(4):
                            P.add('pe', R.matmul(
                                Oc[r // 2][:, r % 2, :], lhsT=PT[:, r, :], rhs=VcX[:, c, g, :],
                                start=False, stop=(ci == len(chunks) - 1 and r % 2 == 1)),
                                reads=['PT', 'VcX'], writes=['po5' if r < 2 else 'po6'])
                    P.add('act', R.activation(out=OcS[:, 0:2, :], in_=Oc[0], func=AF.Copy),
                          reads=['po5'], writes=['OcS'])
                    P.add('act', R.activation(out=OcS[:, 2:4, :], in_=Oc[1], func=AF.Copy),
                          reads=['po6'], writes=['OcS'])
                    P.add('dve', R.tensor_copy(out=sums[:, 4 * g:4 * g + 4, 0], in_=OcS[:, :, 128]),
                          reads=['OcS'], writes=['sums'])
                    P.add('dve', R.tensor_scalar(out=rsc[:], in0=OcS[:, :, 128], scalar1=1e-30, scalar2=None,
                                                           op0=ALU.max), reads=['OcS'], writes=['rsc'])
                    P.add('dve', R.reciprocal(out=rsc[:], in_=rsc[:]), reads=['rsc'], writes=['rsc'])
                    P.add('dve', R.tensor_scalar(out=imp[:], in0=OcS[:, 0, 64:128], scalar1=rsc[:, 0:1],
                                                           scalar2=None, op0=ALU.mult),
                          reads=['OcS', 'rsc'], writes=['imp'])
                    for r in range(1, 4):
                        P.add('dve', R.scalar_tensor_tensor(out=imp[:], in0=OcS[:, r, 64:128],
                                                                           scalar=rsc[:, r:r + 1], in1=imp[:],
                                                                           op0=ALU.mult, op1=ALU.add),
                              reads=['OcS', 'rsc', 'imp'], writes=['imp'])
                    P.add('dve', R.tensor_copy(out=OsS[:, 4 * g:4 * g + 4, :], in_=OcS[:, :, 0:64]),
                          reads=['OcS'], writes=[('OsS', g)])
                    P.add('dve', R.tensor_tensor(out=score[:], in0=imp[:], in1=selmul[:],
                                                                op=ALU.mult), reads=['imp', 'selmul'], writes=['score'])
                    P.add('dve', R.tensor_tensor(out=score[:], in0=score[:], in1=seladd[:],
                                                                op=ALU.add), reads=['score', 'seladd'], writes=['score'])
                    P.add('dve', R.max(out=m8a[:], in_=score[:]), reads=['score'], writes=['m8a'])
                    P.add('dve', R.match_replace(out=sc2[:], in_to_replace=m8a[:], in_values=score[:],
                                                           imm_value=-1e9), reads=['score', 'm8a'], writes=['sc2'])
                    P.add('dve', R.max(out=m8b[:], in_=sc2[:]), reads=['sc2'], writes=['m8b'])
                    P.add('dve', R.tensor_scalar(out=MB[:], in0=score[:], scalar1=m8b[:, 7:8], scalar2=None,
                                                           op0=ALU.is_ge), reads=['score', 'm8b'], writes=['MB'])
                    P.add('dve', R.tensor_scalar(out=MBb[:], in0=MB[:], scalar1=1.0, scalar2=-NEGM,
                                                           op0=ALU.subtract, op1=ALU.mult), reads=['MB'], writes=['MBb'])
                    P.add('pe', R.transpose(out=pT[0:64, 0, :], in_=MBb[:], identity=identb[:]),
                          reads=['MBb', 'identb'], writes=['pT'])
                    P.add('dve', R.tensor_copy(out=MBT[:], in_=pT[0:64, 0, :]), reads=['pT'], writes=['MBT'])
                    for kt in range(i + 1):
                        masks = [(Esel[0:64, kt, :], MBT[:], ['Esel', 'MBT'])]
                        if kt == i:
                            masks.append((identb[:], cmask[:, 0, :], ['identb', 'cmask']))
                        score_tile(KsT, ('KsT', kt), 128 * kt, g, masks)
                        if kt == 0:
                            P.add('pe', R.matmul(po5[:, 0:260], lhsT=zt[:, 0:128], rhs=zt[:, 0:260], start=True, stop=False),
                                  reads=['zt'], writes=['po5'])
                        for r in range(4):
                            P.add('pe', R.matmul(Os[:, r, :], lhsT=PT[:, r, :], rhs=Vs[:, kt, g, :],
                                                 start=False, stop=(kt == i and r == 3)),
                                  reads=['PT', ('Vs', kt)], writes=['po5'])
                    kts = list(range(max(0, i - 4), i + 1))
                    for kt in kts:
                        masks = []
                        if kt == i - 4:
                            masks.append((identb[:], cmask[:, 1, :], ['identb', 'cmask']))
                        if kt == i:
                            masks.append((identb[:], cmask[:, 0, :], ['identb', 'cmask']))
                        score_tile(KwT, ('KwT', kt % 5), 128 * (kt % 5), g, masks)
                        if kt == kts[0]:
                            P.add('pe', R.matmul(po6[:, 0:260], lhsT=zt[:, 0:128], rhs=zt[:, 0:260], start=True, stop=False),
                                  reads=['zt'], writes=['po6'])
                        for r in range(4):
                            P.add('pe', R.matmul(Ow[:, r, :], lhsT=PT[:, r, :], rhs=Vw[:, kt % 5, g, :],
                                                 start=False, stop=(kt == i and r == 3)),
                                  reads=['PT', ('Vw', kt % 5)], writes=['po6'])
                    P.add('dve', R.tensor_copy(out=sums[:, 4 * g:4 * g + 4, 1], in_=Os[:, :, 64]),
                          reads=['po5'], writes=['sums'])
                    P.add('dve', R.tensor_copy(out=sums[:, 4 * g:4 * g + 4, 2], in_=Ow[:, :, 64]),
                          reads=['po6'], writes=['sums'])
                    P.add('dve', R.tensor_scalar(out=coef[:, 4 * g:4 * g + 4, :],
                                                                in0=sums[:, 4 * g:4 * g + 4, :], scalar1=1e-30,
                                                                scalar2=None, op0=ALU.max),
                          reads=['sums'], writes=['coef'])
                    P.add('dve', R.reciprocal(out=coef[:, 4 * g:4 * g + 4, :],
                                                             in_=coef[:, 4 * g:4 * g + 4, :]),
                          reads=['coef'], writes=['coef'])
                    P.add('dve', R.tensor_tensor(
                        out=coef[:, 4 * g:4 * g + 4, :], in0=coef[:, 4 * g:4 * g + 4, :],
                        in1=gsig[:, 12 * g:12 * g + 12].rearrange("p (h b) -> p h b", h=4), op=ALU.mult),
                        reads=['coef', 'gsig'], writes=['coef'])
                    for r in range(4):
                        hh = 4 * g + r
                        dst = mix[:, 512 + 64 * hh:512 + 64 * hh + 64]
                        P.add('dve', R.tensor_scalar(
                            out=dst, in0=OsS[:, hh, :], scalar1=coef[:, hh, 0:1], scalar2=None, op0=ALU.mult),
                            reads=[('OsS', g), 'coef'], writes=['mix'])
                        P.add('dve', R.scalar_tensor_tensor(
                            out=dst, in0=Os[:, r, 0:64], scalar=coef[:, hh, 1:2], in1=dst, op0=ALU.mult, op1=ALU.add),
                            reads=['po5', 'coef', 'mix'], writes=['mix'])
                        P.add('dve', R.scalar_tensor_tensor(
                            out=dst, in0=Ow[:, r, 0:64], scalar=coef[:, hh, 2:3], in1=dst, op0=ALU.mult, op1=ALU.add),
                            reads=['po6', 'coef', 'mix'], writes=['mix'])

                P.add('act', R.activation(out=mixb[:], in_=mix[:], func=AF.Copy), reads=['mix'], writes=['mixb'])
                for kc in range(8):
                    P.add('pe', R.transpose(out=pT[:, kc, :], in_=mixb[:, kc * 128:(kc + 1) * 128],
                                                             identity=identb[:]),
                          reads=['mixb', 'identb'], writes=['pT'])
                P.add('dve', R.tensor_copy(out=mixT[:], in_=pT[:]), reads=['pT'], writes=['mixT'])
                for half in range(2):
                    for kc in range(8):
                        P.add('pe', R.matmul(
                            pz[:, 0:512], lhsT=mixT[:, kc, :], rhs=w_out[:, kc, 512 * half:512 * half + 512],
                            start=(kc == 0), stop=(kc == 7)), reads=['mixT', 'w_out'], writes=['pz'])
                    P.add('dve', R.tensor_tensor(
                        out=x1t[:, 512 * half:512 * half + 512], in0=pz[:, 0:512],
                        in1=xt[:, 512 * half:512 * half + 512], op=ALU.add), reads=['pz', 'xt'], writes=['x1t'])
                x1dst = x1d if 'B' in phases else y
                P.add('sp', R.dma_start(out=x1dst[b, t0:t0 + 128, :], in_=x1t[:]),
                      reads=['x1t'], writes=[('x1d', (b, i))], dma_key='x1st')
        P.barrier()
        A.close()

    if 'B' in phases:
        B = contextlib.ExitStack()
        sb, ps = mk(B)
        w_up = sb('w_up_sb', [128, 8, 4096], BF16)
        w_dn = sb('w_dn_sb', [128, 16, D], BF16)
        w_pg = sb('w_pg_sb', [128, 8, D], BF16)
        w_pp = sb('w_pp_sb', [128, 2, D], BF16)
        g2 = sb('g2_sb', [128, 8])
        g3 = sb('g3_sb', [128, 8])
        cw = sb('cw_sb', [128, 16, 3])
        cb = sb('cb_sb', [128, 16])
        gf = sb('gf_sb', [128, D])
        identb = sb('identbB', [128, 128], BF16)
        xt = sb('x1in', [128, D])
        junk = sb('junkB', [128, D])
        ssq = sb('ssqB', [128, 1])
        rstd = sb('rstdB', [128, 1])
        hb = sb('hbB', [128, D], BF16)
        hT = sb('hTB', [128, 8, 128], BF16)
        gT = sb('gT', [128, 16, 130])
        cv = sb('cv', [128, 128])
        sg = sb('sg', [128, 128])
        aT = sb('aT', [128, 16, 128], BF16)
        x2 = sb('x2', [128, D])
        pt_ = sb('p_t', [128, 256])
        pb = sb('p_b', [128, 256], BF16)
        pTs = sb('pTs', [128, 2, 128], BF16)
        gate = sb('gateB', [128, D])
        x3 = sb('x3', [128, D])
        ot = sb('ot', [128, D])
        pT = ps('pTB', [128, 8, 128], BF16)
        pu = ps('pu', [128, 512])
        pu1 = ps('pu1', [128, 512])
        pd = ps('pd', [128, 512])
        for kc in range(8):
            load_cast(w_up[:, kc, :], dr['w_up'][kc * 128:(kc + 1) * 128, :], 'w_up')
            load_cast(w_pg[:, kc, :], dr['w_pg'][kc * 128:(kc + 1) * 128, :], 'w_pg')
        for fc in range(16):
            load_cast(w_dn[:, fc, :], dr['w_down'][fc * 128:(fc + 1) * 128, :], 'w_dn')
        for kc in range(2):
            load_cast(w_pp[:, kc, :], dr['w_pp'][kc * 128:(kc + 1) * 128, :], 'w_pp')
        load('sp', g2[:], dr['ln2_g'], 'g2')
        load('sp', g3[:], dr['pln_g'], 'g3')
        load('sp', cw[:], dr['conv_w'], 'cw')
        load('sp', cb[:], dr['conv_b'], 'cb')
        load('sp', gf[:], dr['final_g'].partition_broadcast(128), 'gf')
        load_cast(identb[:], dr['c_ident'], 'identbB')

        def norm_T(src, srck, gsb, gk):
            P.add('act', R.activation(out=junk[:], in_=src[:], func=AF.Square, accum_out=ssq[:]),
                  reads=[srck], writes=['junkB', 'ssqB'])
            P.add('act', R.activation(out=rstd[:], in_=ssq[:], func=AF.Ln, bias=EPS, scale=1.0 / D),
                  reads=['ssqB'], writes=['rstdB'])
            P.add('act', R.activation(out=rstd[:], in_=rstd[:], func=AF.Exp, scale=-0.5),
                  reads=['rstdB'], writes=['rstdB'])
            if gsb is None:
                return
            P.add('dve', R.tensor_scalar(out=hb[:], in0=src[:], scalar1=rstd[:, 0:1], scalar2=None,
                                                   op0=ALU.mult), reads=[srck, 'rstdB'], writes=['hbB'])
            for kc in range(8):
                P.add('pe', R.transpose(out=pT[:, kc, :], in_=hb[:, kc * 128:(kc + 1) * 128],
                                                         identity=identb[:]),
                      reads=['hbB', 'identbB'], writes=['pTB'])
            for kc in range(8):
                P.add('dve', R.tensor_scalar(out=hT[:, kc, :], in0=pT[:, kc, :],
                                                              scalar1=gsb[:, kc:kc + 1], scalar2=None, op0=ALU.mult),
                      reads=['pTB', gk], writes=['hTB'])

        for b in range(NB):
            P.add('dve', R.memset(gT[:], 0.0), writes=['gT'])
            for i in range(NT):
                t0 = 128 * i
                x1src = x1d if 'A' in phases else x
                P.add('sp', R.dma_start(out=xt[:], in_=x1src[b, t0:t0 + 128, :]),
                      reads=[('x1d', (b, i))], writes=['x1in'], dma_key='x1in')
                load('sp', pt_[:], pin[b, t0:t0 + 128, :], 'p_t')
                norm_T(xt, 'x1in', g2, 'g2')
                P.add('dve', R.tensor_copy(out=gT[:, :, 0:2], in_=gT[:, :, 128:130]),
                      reads=['gT'], writes=['gT'])
                for fc in range(16):
                    for kc in range(8):
                        P.add('pe', R.matmul(
                            pu[:, 0:128], lhsT=w_up[:, kc, 128 * fc:128 * fc + 128], rhs=hT[:, kc, :],
                            start=(kc == 0), stop=(kc == 7)), reads=['hTB', 'w_up'], writes=['pu'])
                    P.add('act', R.activation(out=gT[:, fc, 2:130], in_=pu[:, 0:128], func=AF.Copy),
                          reads=['pu'], writes=['gT'])
                    for kc in range(8):
                        P.add('pe', R.matmul(
                            pu1[:, 0:128], lhsT=w_up[:, kc, 2048 + 128 * fc:2048 + 128 * fc + 128], rhs=hT[:, kc, :],
                            start=(kc == 0), stop=(kc == 7)), reads=['hTB', 'w_up'], writes=['pu1'])
                    P.add('dve', R.tensor_scalar(out=cv[:], in0=gT[:, fc, 2:130],
                                                                  scalar1=cw[:, fc, 2:3], scalar2=cb[:, fc:fc + 1],
                                                                  op0=ALU.mult, op1=ALU.add),
                          reads=['gT', 'cw', 'cb'], writes=['cv'])
                    P.add('dve', R.scalar_tensor_tensor(out=cv[:], in0=gT[:, fc, 1:129],
                                                                         scalar=cw[:, fc, 1:2], in1=cv[:],
                                                                         op0=ALU.mult, op1=ALU.add),
                          reads=['gT', 'cw', 'cv'], writes=['cv'])
                    P.add('dve', R.scalar_tensor_tensor(out=cv[:], in0=gT[:, fc, 0:128],
                                                                         scalar=cw[:, fc, 0:1], in1=cv[:],
                                                                         op0=ALU.mult, op1=ALU.add),
                          reads=['gT', 'cw', 'cv'], writes=['cv'])
                    P.add('act', R.activation(out=sg[:], in_=cv[:], func=AF.Silu), reads=['cv'], writes=['sg'])
                    P.add('dve', R.tensor_tensor(out=aT[:, fc, :], in0=sg[:], in1=pu1[:, 0:128],
                                                                  op=ALU.mult),
                          reads=['sg', 'pu1'], writes=['aT'])
                for half in range(2):
                    for fc in range(16):
                        P.add('pe', R.matmul(
                            pd[:, 0:512], lhsT=aT[:, fc, :], rhs=w_dn[:, fc, 512 * half:512 * half + 512],
                            start=(fc == 0), stop=(fc == 15)), reads=['aT', 'w_dn'], writes=['pd'])
                    P.add('dve', R.tensor_tensor(
                        out=x2[:, 512 * half:512 * half + 512], in0=pd[:, 0:512],
                        in1=xt[:, 512 * half:512 * half + 512], op=ALU.add), reads=['pd', 'x1in'], writes=['x2'])
                norm_T(x2, 'x2', g3, 'g3')
                P.add('dve', R.tensor_copy(out=pb[:], in_=pt_[:]), reads=['p_t'], writes=['p_b'])
                for kc in range(2):
                    P.add('pe', R.transpose(out=pT[:, kc, :], in_=pb[:, kc * 128:(kc + 1) * 128],
                                                             identity=identb[:]),
                          reads=['p_b', 'identbB', 'hTB'], writes=['pTB'])
                P.add('dve', R.tensor_copy(out=pTs[:], in_=pT[:, 0:2, :]), reads=['pTB'], writes=['pTs'])
                for half in range(2):
                    cs = slice(512 * half, 512 * half + 512)
                    for kc in range(8):
                        P.add('pe', R.matmul(pd[:, 0:512], lhsT=hT[:, kc, :], rhs=w_pg[:, kc, cs],
                                                                     start=(kc == 0), stop=(kc == 7)),
                              reads=['hTB', 'w_pg'], writes=['pd'])
                    P.add('act', R.activation(out=gate[:, cs], in_=pd[:, 0:512], func=AF.Sigmoid),
                          reads=['pd'], writes=['gateB'])
                    for kc in range(2):
                        P.add('pe', R.matmul(pu[:, 0:512], lhsT=pTs[:, kc, :], rhs=w_pp[:, kc, cs],
                                                                     start=(kc == 0), stop=(kc == 1)),
                              reads=['pTs', 'w_pp'], writes=['pu'])
                    P.add('dve', R.tensor_tensor(out=gate[:, cs], in0=gate[:, cs], in1=pu[:, 0:512],
                                                                  op=ALU.mult), reads=['gateB', 'pu'], writes=['gateB'])
                P.add('dve', R.tensor_tensor(out=x3[:], in0=x2[:], in1=gate[:], op=ALU.add),
                      reads=['x2', 'gateB'], writes=['x3'])
                norm_T(x3, 'x3', None, None)
                P.add('dve', R.scalar_tensor_tensor(out=ot[:], in0=x3[:], scalar=rstd[:, 0:1], in1=gf[:],
                                                              op0=ALU.mult, op1=ALU.mult),
                      reads=['x3', 'rstdB', 'gf'], writes=['ot'])
                P.add('sp', R.dma_start(out=y[b, t0:t0 + 128, :], in_=ot[:]),
                      reads=['ot'], dma_key='yst')
        B.close()
    P.build()
    top.close()
    return nc


def make_in_maps(inputs, S, NB, ncores):
    cs = host_consts(S)
    ws = host_weights(inputs)
    x = np.asarray(inputs['x'], dtype=np.float32)
    p = np.asarray(inputs['p'], dtype=np.float32)[0]
    maps = []
    for c in range(ncores):
        m = {'x': np.ascontiguousarray(x[c * NB:(c + 1) * NB]), 'p': np.ascontiguousarray(p[c * NB:(c + 1) * NB])}
        m.update(cs)
        m.update(ws)
        maps.append(m)
    return maps


def kernel(**inputs):
    S, NB, ncores = 4096, 2, 8
    nc = build_program(S, NB)
    maps = make_in_maps(inputs, S, NB, ncores)
    res = run_bass_kernel_spmd(nc, maps, core_ids=list(range(ncores)))
    return np.concatenate([np.asarray(r['y'], dtype=np.float32) for r in res.results], axis=0)
```
